# Optimizing a Trainium2 kernel written in Bass

```python
import math
import jax, jax.numpy as jnp
from jax import lax
import numpy as np

D_MODEL = 2048
BATCH = 8
SEQ = 2048
DEPTH = 2

MIX_W = D_MODEL // 2
DIFF_HEADS = 8
DIFF_HEAD_DIM = MIX_W // DIFF_HEADS // 2
DIFF_V_DIM = 2 * DIFF_HEAD_DIM
CONV_CH = MIX_W
CONV_WIDTH = 31
RET_HEADS = 8
RET_HEAD_DIM = MIX_W // RET_HEADS
N_BRANCH = 3
XATTN_HEADS = 4
XATTN_HEAD_DIM = D_MODEL // XATTN_HEADS
MEM_TOKENS = 256
FFN_HIDDEN = -(-8 * D_MODEL // (3 * 256)) * 256
ROPE_THETA = 10000.0
Q_BLOCK = 128
RET_CHUNK = 128
EPS = 1e-6
NEG_INF = -1e30
IN_SPLITS = [MIX_W, MIX_W, MIX_W,
             2 * CONV_CH,
             MIX_W, MIX_W, MIX_W, MIX_W,
             N_BRANCH * D_MODEL]
IN_COLS = sum(IN_SPLITS)

kernel_name = "hybrid_diffattn_conformer_retention_block"


def rmsnorm(x, g):
    xf = x.astype(jnp.float32)
    y = xf * lax.rsqrt(jnp.mean(xf * xf, axis=-1, keepdims=True) + EPS)
    return (y * g.astype(jnp.float32)).astype(x.dtype)


def layernorm_f32(xf, g, b):
    mu = jnp.mean(xf, axis=-1, keepdims=True)
    var = jnp.mean(jnp.square(xf - mu), axis=-1, keepdims=True)
    return (xf - mu) * lax.rsqrt(var + EPS) * g.astype(jnp.float32) + b.astype(jnp.float32)


def rope_tables(T, dim, inv_freq):
    ang = jnp.arange(T, dtype=jnp.float32)[:, None] * inv_freq[None, :]
    return jnp.cos(ang), jnp.sin(ang)


def apply_rope(x, cos, sin):
    shape = (cos.shape[0],) + (1,) * (x.ndim - 3) + (cos.shape[1],)
    c, s = cos.reshape(shape), sin.reshape(shape)
    xf = x.astype(jnp.float32)
    x1, x2 = jnp.split(xf, 2, axis=-1)
    return jnp.concatenate([x1 * c - x2 * s, x2 * c + x1 * s], axis=-1)


def diff_attention(q, k, v, lam):
    B, T, H = q.shape[:3]
    n_blocks = T // Q_BLOCK
    scale = DIFF_HEAD_DIM ** -0.5
    vf = v.astype(jnp.float32)
    k_pos = jnp.arange(T)

    def one_block(blk):
        start = blk * Q_BLOCK
        qb = lax.dynamic_slice_in_dim(q, start, Q_BLOCK, axis=1)
        logits = jnp.einsum('bqhmd,bkhmd->bmhqk', qb, k) * scale
        q_pos = start + jnp.arange(Q_BLOCK)
        causal = k_pos[None, :] <= q_pos[:, None]
        probs = jax.nn.softmax(jnp.where(causal, logits, NEG_INF), axis=-1)
        weights = probs[:, 0] - lam * probs[:, 1]
        return jnp.einsum('bhqk,bkhe->bqhe', weights, vf)

    out = lax.map(one_block, jnp.arange(n_blocks))
    return jnp.moveaxis(out, 0, 1).reshape(B, T, H, vf.shape[-1])


def retention_chunkwise(q, k, v, log_gamma):
    B, T, H, dk = q.shape
    dv = v.shape[-1]
    N = T // RET_CHUNK
    q = q.reshape(B, N, RET_CHUNK, H, dk) * dk ** -0.5
    k = k.reshape(B, N, RET_CHUNK, H, dk)
    v = v.astype(jnp.float32).reshape(B, N, RET_CHUNK, H, dv)
    idx = jnp.arange(RET_CHUNK, dtype=jnp.float32)
    rel = idx[:, None] - idx[None, :]
    intra_decay = jnp.where(rel >= 0, jnp.exp(log_gamma[:, None, None] * jnp.maximum(rel, 0.0)), 0.0)
    scores = jnp.einsum('bnihd,bnjhd->bnhij', q, k) * intra_decay
    intra = jnp.einsum('bnhij,bnjhe->bnihe', scores, v)
    k_decay = jnp.exp(log_gamma[None, :] * (RET_CHUNK - 1 - idx)[:, None])
    chunk_kv = jnp.einsum('bnjhd,jh,bnjhe->nbhde', k, k_decay, v)
    chunk_decay = jnp.exp(log_gamma * RET_CHUNK)[:, None, None]

    def step(state, kv_n):
        return state * chunk_decay + kv_n, state

    _, states = lax.scan(step, jnp.zeros((B, H, dk, dv), jnp.float32), chunk_kv)
    q_decay = jnp.exp(log_gamma[None, :] * (idx + 1.0)[:, None])
    cross = jnp.einsum('bnihd,ih,nbhde->bnihe', q, q_decay, states)
    return (intra + cross).reshape(B, T, H, dv)


def hybrid_mixer(h, layer_idx, w_in, diff_lambda, diff_subln, conv_w, conv_b, conv_ln_g, conv_ln_b,
                 ret_gn_g, w_branch, w_out, diff_cos, diff_sin, ret_cos, ret_sin):
    B, T, D = h.shape
    proj = h @ w_in
    dq, dk, dv, cin, rq, rk, rv, rg, gates = jnp.split(proj, list(np.cumsum(IN_SPLITS)[:-1]), axis=-1)

    q = apply_rope(dq.reshape(B, T, DIFF_HEADS, 2, DIFF_HEAD_DIM), diff_cos, diff_sin)
    k = apply_rope(dk.reshape(B, T, DIFF_HEADS, 2, DIFF_HEAD_DIM), diff_cos, diff_sin)
    lam_f = diff_lambda.astype(jnp.float32)
    lambda_init = 0.8 - 0.6 * math.exp(-0.3 * layer_idx)
    lam = (jnp.exp(jnp.sum(lam_f[0] * lam_f[1])) - jnp.exp(jnp.sum(lam_f[2] * lam_f[3])) + lambda_init)
    ya = diff_attention(q, k, dv.reshape(B, T, DIFF_HEADS, DIFF_V_DIM), lam)
    ya = ya * lax.rsqrt(jnp.mean(ya * ya, axis=-1, keepdims=True) + EPS) * diff_subln.astype(jnp.float32)
    ya = (ya * (1.0 - lambda_init)).reshape(B, T, MIX_W).astype(h.dtype)

    ca, cb = jnp.split(cin, 2, axis=-1)
    u = ca * jax.nn.sigmoid(cb)
    yc = lax.conv_general_dilated(u, conv_w[:, None, :].astype(u.dtype), window_strides=(1,),
                                  padding=[(CONV_WIDTH - 1, 0)],
                                  dimension_numbers=('NWC', 'WIO', 'NWC'),
                                  feature_group_count=CONV_CH)
    yc = layernorm_f32(yc.astype(jnp.float32) + conv_b.astype(jnp.float32), conv_ln_g, conv_ln_b)
    yb = jax.nn.silu(yc).astype(h.dtype)

    log_gamma = jnp.log1p(-jnp.exp2(-5.0 - jnp.arange(RET_HEADS, dtype=jnp.float32)))
    rqr = apply_rope(rq.reshape(B, T, RET_HEADS, RET_HEAD_DIM), ret_cos, ret_sin)
    rkr = apply_rope(rk.reshape(B, T, RET_HEADS, RET_HEAD_DIM), ret_cos, ret_sin)
    yr = retention_chunkwise(rqr, rkr, rv.reshape(B, T, RET_HEADS, RET_HEAD_DIM), log_gamma)
    mu = jnp.mean(yr, axis=-1, keepdims=True)
    var = jnp.mean(jnp.square(yr - mu), axis=-1, keepdims=True)
    yr = ((yr - mu) * lax.rsqrt(var + EPS)).reshape(B, T, MIX_W) * ret_gn_g.astype(jnp.float32)
    yr = (jax.nn.silu(rg.astype(jnp.float32)) * yr).astype(h.dtype)

    g = jax.nn.sigmoid(gates.astype(jnp.float32)).reshape(B, T, N_BRANCH, D)
    merged = (g[:, :, 0] * (ya @ w_branch[0]).astype(jnp.float32)
              + g[:, :, 1] * (yb @ w_branch[1]).astype(jnp.float32)
              + g[:, :, 2] * (yr @ w_branch[2]).astype(jnp.float32))
    return merged.astype(h.dtype) @ w_out


def cross_attention(h, m, wq, wkv, wo):
    B, T, D = h.shape
    M = m.shape[1]
    q = (h @ wq).reshape(B, T, XATTN_HEADS, XATTN_HEAD_DIM).astype(jnp.float32)
    k, v = jnp.split(m @ wkv, 2, axis=-1)
    k = k.reshape(B, M, XATTN_HEADS, XATTN_HEAD_DIM).astype(jnp.float32)
    v = v.reshape(B, M, XATTN_HEADS, XATTN_HEAD_DIM).astype(jnp.float32)
    p = jax.nn.softmax(jnp.einsum('bthd,bmhd->bhtm', q, k) * XATTN_HEAD_DIM ** -0.5, axis=-1)
    o = jnp.einsum('bhtm,bmhd->bthd', p, v).reshape(B, T, D).astype(h.dtype)
    return o @ wo


def swiglu(h, w13, w2):
    a, b = jnp.split(h @ w13, 2, axis=-1)
    return (jax.nn.silu(a.astype(jnp.float32)) * b.astype(jnp.float32)).astype(h.dtype) @ w2


def setup_inputs(seed: int = 0) -> dict:
    key = jax.random.key(seed)
    ks = jax.random.split(key, 24)
    f32 = jnp.float32

    def nrm(k, shape, fan_in):
        return jax.random.normal(k, shape, f32) * fan_in ** -0.5

    def gain(k, shape):
        return 1.0 + 0.02 * jax.random.normal(k, shape, f32)

    return {
        "x": jax.random.normal(ks[0], (BATCH, SEQ, D_MODEL), f32),
        "mem": jax.random.normal(ks[1], (BATCH, MEM_TOKENS, D_MODEL), f32),
        "norm_mix": gain(ks[2], (DEPTH, D_MODEL)),
        "w_in": nrm(ks[3], (DEPTH, D_MODEL, IN_COLS), D_MODEL),
        "diff_lambda": 0.1 * jax.random.normal(ks[4], (DEPTH, 4, DIFF_HEAD_DIM), f32),
        "diff_subln": gain(ks[5], (DEPTH, DIFF_V_DIM)),
        "conv_w": nrm(ks[6], (DEPTH, CONV_WIDTH, CONV_CH), CONV_WIDTH),
        "conv_b": 0.02 * jax.random.normal(ks[7], (DEPTH, CONV_CH), f32),
        "conv_ln_g": gain(ks[8], (DEPTH, CONV_CH)),
        "conv_ln_b": 0.02 * jax.random.normal(ks[9], (DEPTH, CONV_CH), f32),
        "ret_gn_g": gain(ks[10], (DEPTH, MIX_W)),
        "w_branch": nrm(ks[11], (DEPTH, N_BRANCH, MIX_W, D_MODEL), MIX_W),
        "w_out": nrm(ks[12], (DEPTH, D_MODEL, D_MODEL), D_MODEL),
        "norm_xattn": gain(ks[13], (DEPTH, D_MODEL)),
        "norm_mem": gain(ks[14], (DEPTH, D_MODEL)),
        "xattn_wq": nrm(ks[15], (DEPTH, D_MODEL, D_MODEL), D_MODEL),
        "xattn_wkv": nrm(ks[16], (DEPTH, D_MODEL, 2 * D_MODEL), D_MODEL),
        "xattn_wo": nrm(ks[17], (DEPTH, D_MODEL, D_MODEL), D_MODEL),
        "norm_ffn": gain(ks[18], (DEPTH, D_MODEL)),
        "ffn_w13": nrm(ks[19], (DEPTH, D_MODEL, 2 * FFN_HIDDEN), D_MODEL),
        "ffn_w2": nrm(ks[20], (DEPTH, FFN_HIDDEN, D_MODEL), FFN_HIDDEN),
        "norm_final": gain(ks[21], (D_MODEL,)),
    }


def reference(x, mem, norm_mix, w_in, diff_lambda, diff_subln, conv_w, conv_b, conv_ln_g, conv_ln_b,
              ret_gn_g, w_branch, w_out, norm_xattn, norm_mem, xattn_wq, xattn_wkv, xattn_wo,
              norm_ffn, ffn_w13, ffn_w2, norm_final):
    T = x.shape[1]
    diff_inv = ROPE_THETA ** (-jnp.arange(0, DIFF_HEAD_DIM, 2, dtype=jnp.float32) / DIFF_HEAD_DIM)
    diff_cos, diff_sin = rope_tables(T, DIFF_HEAD_DIM, diff_inv)
    ret_inv = 1.0 / (ROPE_THETA ** jnp.linspace(0.0, 1.0, RET_HEAD_DIM // 2, dtype=jnp.float32))
    ret_cos, ret_sin = rope_tables(T, RET_HEAD_DIM, ret_inv)

    for i in range(DEPTH):
        h = rmsnorm(x, norm_mix[i])
        x = x + hybrid_mixer(h, i, w_in[i], diff_lambda[i], diff_subln[i], conv_w[i], conv_b[i],
                             conv_ln_g[i], conv_ln_b[i], ret_gn_g[i], w_branch[i], w_out[i],
                             diff_cos, diff_sin, ret_cos, ret_sin)
        x = x + cross_attention(rmsnorm(x, norm_xattn[i]), rmsnorm(mem, norm_mem[i]),
                                xattn_wq[i], xattn_wkv[i], xattn_wo[i])
        x = x + swiglu(rmsnorm(x, norm_ffn[i]), ffn_w13[i], ffn_w2[i])
    return rmsnorm(x, norm_final)
```

```python
import math
import contextlib
import numpy as np
import concourse.bass as bass
import concourse.mybir as mybir
from concourse.bass_utils import run_bass_kernel_spmd

F32 = mybir.dt.float32
BF16 = mybir.dt.bfloat16
AF = mybir.ActivationFunctionType
ALU = mybir.AluOpType

COMPUTE = ("pe", "act", "dve", "pool")
ENGS = ("pe", "act", "dve", "pool", "sp")
NDSEM = {"sp": 16, "pool": 16}

T = 2048
D = 2048
DEPTH = 2
MIXW = 1024
FFN = 5632
INCOLS = 15360
MEMT = 256
EPS = 1e-6


class Buf:
    __slots__ = ("w", "r")

    def __init__(self):
        self.w = None
        self.r = []


class Op:
    __slots__ = ("eng", "fn", "dma", "pos", "waits", "signal", "sigval", "clock",
                 "dslot", "dval", "didx")

    def __init__(self, eng, fn, dma):
        self.eng = eng
        self.fn = fn
        self.dma = dma
        self.waits = []
        self.signal = False
        self.sigval = 0
        self.clock = None
        self.didx = -1


class Prog:
    def __init__(self, nc):
        self.nc = nc
        self.ops = {e: [] for e in ENGS}
        self.clock = {e: {f: -1 for f in COMPUTE} for e in ENGS}
        self.dret = {e: {q: -1 for q in NDSEM} for e in ENGS}
        self.dwaited = {e: set() for e in ENGS}
        self.dma_ops = {q: [] for q in NDSEM}
        self.lastc = {e: None for e in COMPUTE}

    def op(self, eng, fn, reads=(), writes=(), dma=False):
        o = Op(eng, fn, dma)
        lst = self.ops[eng]
        o.pos = len(lst)
        deps = set()
        for b in reads:
            if b.w is not None:
                deps.add(b.w)
        for b in writes:
            if b.w is not None:
                deps.add(b.w)
            for r in b.r:
                deps.add(r)
        if dma:
            dl = self.dma_ops[eng]
            o.didx = len(dl)
            K = NDSEM[eng]
            o.dslot = o.didx % K
            o.dval = 16 * (o.didx // K + 1)
            if o.didx >= K:
                deps.add(dl[o.didx - K])
            dl.append(o)
        clk = self.clock[eng]
        dw = self.dwaited[eng]
        dret = self.dret[eng]
        best = {}
        for d in deps:
            if d is o:
                continue
            if d.dma:
                if d.didx <= dret[d.eng] or d in dw:
                    continue
                dw.add(d)
                o.waits.append(d)
                for f, p in d.clock.items():
                    if p > clk[f]:
                        clk[f] = p
            else:
                f = d.eng
                if f == eng:
                    if eng == "pe":
                        continue
                    if d.pos < o.pos - 2 or d.pos <= clk[f]:
                        continue
                elif d.pos <= clk[f]:
                    continue
                if f not in best or d.pos > best[f].pos:
                    best[f] = d
        for f, d in best.items():
            if d.pos <= clk[f]:
                continue
            d.signal = True
            o.waits.append(d)
            clk[f] = d.pos
            for g, p in d.clock.items():
                if g != eng and p > clk[g]:
                    clk[g] = p
        snap = dict(clk)
        if (not dma) and eng in COMPUTE:
            snap[eng] = o.pos
            self.lastc[eng] = o
        o.clock = snap
        for b in writes:
            b.w = o
            b.r = []
        for b in reads:
            if b.w is not o:
                b.r.append(o)
        lst.append(o)
        return o

    def barrier(self):
        lasts = dict(self.lastc)
        ndma = {q: len(self.dma_ops[q]) for q in NDSEM}
        for e in ENGS:
            o = Op(e, None, False)
            o.pos = len(self.ops[e])
            clk = self.clock[e]
            for f in COMPUTE:
                d = lasts[f]
                if f == e or d is None or d.pos <= clk[f]:
                    continue
                d.signal = True
                o.waits.append(d)
                clk[f] = d.pos
            for q, K in NDSEM.items():
                n = ndma[q]
                for j in range(max(0, n - K), n):
                    d = self.dma_ops[q][j]
                    if d.didx > self.dret[e][q] and d not in self.dwaited[e]:
                        o.waits.append(d)
                self.dret[e][q] = n - 1
            self.dwaited[e] = set()
            o.clock = dict(clk)
            self.ops[e].append(o)
        for e in ENGS:
            for f in COMPUTE:
                if lasts[f] is not None and f != e:
                    self.clock[e][f] = max(self.clock[e][f], lasts[f].pos)

    def emit(self):
        nc = self.nc
        with contextlib.ExitStack() as st:
            csem = {e: st.enter_context(nc.semaphore("c_" + e)) for e in COMPUTE}
            dsem = {e: [st.enter_context(nc.semaphore("d_%s%d" % (e, i))) for i in range(K)]
                    for e, K in NDSEM.items()}
            for e in COMPUTE:
                n = 0
                for o in self.ops[e]:
                    if o.signal:
                        n += 1
                        o.sigval = n
            block = st.enter_context(nc.Block())

            def run(engname, eng):
                for o in self.ops[engname]:
                    for d in o.waits:
                        if d.dma:
                            eng.wait_ge(dsem[d.eng][d.dslot], d.dval)
                        else:
                            eng.wait_ge(csem[d.eng], d.sigval)
                    if o.fn is None:
                        continue
                    ins = o.fn(eng)
                    if o.dma:
                        ins.then_inc(dsem[engname][o.dslot], 16)
                    elif o.signal:
                        ins.then_inc(csem[engname], 1)
                if engname in NDSEM:
                    K = NDSEM[engname]
                    n = len(self.dma_ops[engname])
                    for s in range(K):
                        cnt = (n - s + K - 1) // K if n > s else 0
                        if cnt > 0:
                            eng.wait_ge(dsem[engname][s], 16 * cnt)

            @block.tensor
            def _(eng):
                run("pe", eng)

            @block.scalar
            def _(eng):
                run("act", eng)

            @block.vector
            def _(eng):
                run("dve", eng)

            @block.gpsimd
            def _(eng):
                run("pool", eng)

            @block.sync
            def _(eng):
                run("sp", eng)


class Tile:
    def __init__(self, ap, nb=1):
        self.ap = ap
        self.bufs = [Buf() for _ in range(nb)]

    @property
    def buf(self):
        return self.bufs[0]


class Rot:
    def __init__(self, tiles):
        self.tiles = tiles
        self.i = 0

    def next(self):
        t = self.tiles[self.i % len(self.tiles)]
        self.i += 1
        return t


LAMBDA_INIT = [0.8 - 0.6 * math.exp(-0.3 * i) for i in range(DEPTH)]
_LG = np.log1p(-np.exp2(-5.0 - np.arange(8, dtype=np.float32))).astype(np.float32)
CHUNK_DECAY = [float(np.exp(np.float32(_LG[h] * np.float32(128.0)))) for h in range(8)]

C_COSD, C_SIND, C_COSR, C_SINR = 0, 2048, 4096, 6144
C_DECT = 8192
C_QDEC = C_DECT + 1024
C_KDEC = C_QDEC + 1024
C_BF = C_KDEC + 8
NCST = C_BF + 640
V_NMIX, V_NXA, V_NMEM, V_NFFN, V_CONVW = 0, 16, 32, 48, 64
V_CONVB = V_CONVW + 248
V_LNG = V_CONVB + 8
V_LNB = V_LNG + 8
V_GNG = V_LNB + 8
V_SUBLN = V_GNG + 8
V_NFIN = V_SUBLN + 1
NVEC = V_NFIN + 16


def host_consts():
    f32 = np.float32
    cst = np.zeros((128, NCST), f32)
    t = np.arange(T, dtype=f32)
    p = np.arange(128)
    diff_inv = (f32(10000.0) ** (-np.arange(0, 64, 2, dtype=f32) / f32(64))).astype(f32)
    ang = (t[None, :] * diff_inv[p % 32][:, None]).astype(f32)
    sgn = np.where((p % 64) < 32, -1.0, 1.0).astype(f32)[:, None]
    cst[:, C_COSD:C_COSD + T] = np.cos(ang)
    cst[:, C_SIND:C_SIND + T] = np.sin(ang) * sgn
    ret_inv = (f32(1.0) / (f32(10000.0) ** np.linspace(0.0, 1.0, 64, dtype=f32))).astype(f32)
    ang = (t[None, :] * ret_inv[p % 64][:, None]).astype(f32)
    sgn = np.where(p < 64, -1.0, 1.0).astype(f32)[:, None]
    cst[:, C_COSR:C_COSR + T] = np.cos(ang)
    cst[:, C_SINR:C_SINR + T] = np.sin(ang) * sgn
    idx = np.arange(128, dtype=f32)
    scale = f32(128.0 ** -0.5)
    for h in range(8):
        rel = idx[None, :] - idx[:, None]
        dec = np.where(rel >= 0, np.exp(_LG[h] * np.maximum(rel, 0.0)), 0.0).astype(f32)
        cst[:, C_DECT + h * 128:C_DECT + (h + 1) * 128] = dec * scale
        cst[:, C_QDEC + h * 128:C_QDEC + (h + 1) * 128] = (np.exp(_LG[h] * (idx + 1.0)) * scale)[None, :]
        cst[:, C_KDEC + h] = np.exp(_LG[h] * (127.0 - idx))
    q = np.arange(128)
    cst[:, C_BF + 0:C_BF + 128] = (q[:, None] == (q[None, :] ^ 32))
    cst[:, C_BF + 128:C_BF + 256] = (q[:, None] == (q[None, :] ^ 64))
    cst[:, C_BF + 256:C_BF + 384] = np.eye(128)
    cst[:, C_BF + 384:C_BF + 512] = 1.0
    cst[:, C_BF + 512:C_BF + 640] = (q[None, :] >= q[:, None])
    return cst


class KB:
    def __init__(self, nc, depth=DEPTH, dbg=None):
        self.nc = nc
        self.P = Prog(nc)
        self.depth = depth
        self.dbg = dbg or {}
        self.arena = nc.alloc_sbuf_tensor("arena", [128, 53000], F32)
        pst = nc.alloc_psum_tensor("ps", [128, 8, 512], F32)
        self.psb = [Tile(pst[:, i, :]) for i in range(8)]
        self.psi = 0
        skip = self.dbg.get("skip_inputs", ())

        def dt(name, shape, dty, kind="Internal"):
            if name in skip:
                kind = "Internal"
            return nc.dram_tensor(name, shape, dty, kind=kind).ap()
        self.dt = dt
        self.xT = dt("xT", [D, T], F32, "ExternalInput")
        self.memT = dt("memT", [D, MEMT], F32, "ExternalInput")
        self.cst = dt("cst", [128, NCST], F32, "ExternalInput")
        self.vecs = dt("vecs", [DEPTH, 128, NVEC], F32, "ExternalInput")
        self.lamin = dt("lamin", [DEPTH, 128, 256], F32, "ExternalInput")
        self.w_in = dt("w_in", [DEPTH, D, INCOLS], F32, "ExternalInput")
        self.w_branch = dt("w_branch", [DEPTH, 3 * MIXW, D], F32, "ExternalInput")
        self.w_out = dt("w_out", [DEPTH, D, D], F32, "ExternalInput")
        self.wq = dt("xattn_wq", [DEPTH, D, D], F32, "ExternalInput")
        self.wkv = dt("xattn_wkv", [DEPTH, D, 2 * D], F32, "ExternalInput")
        self.wo = dt("xattn_wo", [DEPTH, D, D], F32, "ExternalInput")
        self.w13 = dt("ffn_w13", [DEPTH, D, 2 * FFN], F32, "ExternalInput")
        self.w2 = dt("ffn_w2", [DEPTH, FFN, D], F32, "ExternalInput")
        self.yT = dt("yT", [D, T], F32, "ExternalOutput")
        k = lambda n: self.dbg.get(n, "Internal")
        self.qdT = dt("qdT", [MIXW, T], BF16, k("qdT"))
        self.kdT = dt("kdT", [MIXW, T], BF16, k("kdT"))
        self.qrT = dt("qrT", [MIXW, T], BF16, k("qrT"))
        self.krT = dt("krT", [MIXW, T], BF16, k("krT"))
        self.vd = dt("vd", [T, MIXW], BF16, k("vd"))
        self.vr = dt("vr", [T, MIXW], BF16, k("vr"))
        self.ycT = dt("ycT", [MIXW, T], F32, k("ycT"))
        self.sgT = dt("sgT", [MIXW, T], BF16, k("sgT"))
        self.gT = dt("gT", [3 * D, T], BF16, k("gT"))
        self.ybrT = dt("ybrT", [3 * MIXW, T], BF16, k("ybrT"))
        self.xres = dt("xres", [D, T], F32, k("xres"))
        self.hffT = dt("hffT", [FFN, T], BF16, k("hffT"))
        self.cbf = Tile(self.view(0, [128, 640], BF16))
        self.vec = Tile(self.view(1280, [128, NVEC], F32))
        self.lam = Tile(self.view(1280 + 4 * NVEC, [128, 8], F32))
        self.epsb = Tile(self.view(1280 + 4 * NVEC + 32, [128, 2], F32))
        self.base = 4096
        self.permd = self.cbf.ap[:, 0:128]
        self.permr = self.cbf.ap[:, 128:256]
        self.ident = self.cbf.ap[:, 256:384]
        self.ones = self.cbf.ap[:, 384:512]
        self.tri = self.cbf.ap[:, 512:640]

    def view(self, off, shape, dty):
        esz = 4 if dty == F32 else 2
        n = int(np.prod(shape[1:]))
        assert off % 4 == 0 and (n * esz) % 4 == 0
        assert off + n * esz <= 53000 * 4, (off, shape)
        a = self.arena[:, off // 4: off // 4 + (n * esz) // 4]
        if dty != F32:
            a = a.bitcast(dty)
        if len(shape) == 3:
            a = a.rearrange("p (a b) -> p a b", a=shape[1])
        return a

    class Alloc:
        def __init__(self, kb, start):
            self.kb = kb
            self.off = start

        def tile(self, shape, dty, nb=1):
            esz = 4 if dty == F32 else 2
            n = int(np.prod(shape[1:])) * esz
            n = (n + 31) // 32 * 32
            t = Tile(self.kb.view(self.off, shape, dty), nb)
            self.off += n
            return t

        def rot(self, k, shape, dty, nb=1):
            return Rot([self.tile(shape, dty, nb) for _ in range(k)])

    def alloc(self):
        return KB.Alloc(self, self.base)

    def A(self, eng, fn, r=(), w=()):
        return self.P.op(eng, fn, [b for b in r], [b for b in w])

    def dma(self, q, out, in_, r=(), w=()):
        return self.P.op(q, lambda e: e.dma_start(out=out, in_=in_), list(r), list(w), dma=True)

    def mm(self, out_t, out_ap, lhsT, rhs, start, stop, r=()):
        rd = list(r)
        return self.P.op("pe", lambda e: e.matmul(out_ap, lhsT=lhsT, rhs=rhs, start=start, stop=stop),
                         rd, [out_t.buf])

    def act(self, out, in_, func, r=(), w=(), eng="act", **kw):
        return self.P.op("act", lambda e: e.activation(out=out, in_=in_, func=func, **kw), list(r), list(w))

    def tt(self, eng, out, in0, in1, op, r=(), w=()):
        return self.P.op(eng, lambda e: e.tensor_tensor(out=out, in0=in0, in1=in1, op=op), list(r), list(w))

    def ts(self, eng, out, in0, s1, s2, op0, op1=None, r=(), w=()):
        if op1 is None:
            return self.P.op(eng, lambda e: e.tensor_scalar(out=out, in0=in0, scalar1=s1, scalar2=None, op0=op0),
                             list(r), list(w))
        return self.P.op(eng, lambda e: e.tensor_scalar(out=out, in0=in0, scalar1=s1, scalar2=s2, op0=op0, op1=op1),
                         list(r), list(w))

    def stt(self, eng, out, in0, scalar, in1, op0, op1, r=(), w=()):
        return self.P.op(eng, lambda e: e.scalar_tensor_tensor(out=out, in0=in0, scalar=scalar, in1=in1,
                                                               op0=op0, op1=op1), list(r), list(w))

    def copy(self, eng, out, in_, r=(), w=()):
        if eng == "act":
            return self.P.op("act", lambda e: e.copy(out=out, in_=in_), list(r), list(w))
        return self.P.op(eng, lambda e: e.tensor_copy(out=out, in_=in_), list(r), list(w))

    def nextps(self, lo=0, hi=6):
        t = self.psb[lo + self.psi % (hi - lo)]
        self.psi += 1
        return t

    def setup(self):
        al = self.alloc()
        tmp = al.tile([128, 640], F32)
        self.dma("sp", tmp.ap, self.cst[:, C_BF:C_BF + 640], w=[tmp.buf])
        self.copy("dve", self.cbf.ap, tmp.ap, r=[tmp.buf], w=[self.cbf.buf])
        self.A("pool", lambda e: e.memset(self.epsb.ap, EPS), w=[self.epsb.buf])
        self.P.barrier()

    def layer_setup(self, L):
        al = self.alloc()
        self.dma("sp", self.vec.ap, self.vecs[L], w=[self.vec.buf])
        lt = al.tile([128, 256], F32)
        pr = al.tile([128, 128], F32)
        sm = al.tile([128, 2], F32)
        self.dma("sp", lt.ap, self.lamin[L], w=[lt.buf])
        self.tt("dve", pr.ap, lt.ap[:, 0:128], lt.ap[:, 128:256], ALU.mult, r=[lt.buf], w=[pr.buf])
        self.A("dve", lambda e: e.reduce_sum(out=sm.ap, in_=pr.ap.rearrange("p (a b) -> p a b", a=2),
                                             axis=mybir.AxisListType.X), r=[pr.buf], w=[sm.buf])
        self.act(sm.ap, sm.ap, AF.Exp, r=[sm.buf], w=[sm.buf])
        lam = self.lam
        self.tt("dve", lam.ap[:, 0:1], sm.ap[:, 0:1], sm.ap[:, 1:2], ALU.subtract, r=[sm.buf], w=[lam.buf])
        self.ts("dve", lam.ap[:, 1:2], lam.ap[:, 0:1], float(LAMBDA_INIT[L]), -1.0, ALU.add, ALU.mult,
                r=[lam.buf], w=[lam.buf])
        self.ts("dve", lam.ap[:, 2:3], self.vec.ap[:, V_SUBLN:V_SUBLN + 1], float(1.0 - LAMBDA_INIT[L]), None,
                ALU.mult, r=[self.vec.buf], w=[lam.buf])
        self.P.barrier()

    def norm(self, src, gcol, XT, ntok, al):
        tgs = min(512, ntok)
        ntg = ntok // tgs
        xs = al.rot(2, [128, ntok], F32)
        sq = al.rot(2, [128, ntok], BF16)
        rstd = al.tile([128, ntok], F32, nb=ntg)
        vec = self.vec
        for c in range(16):
            x = xs.next()
            s = sq.next()
            self.dma("sp", x.ap, src[c * 128:(c + 1) * 128, :], w=[x.buf])
            self.act(s.ap, x.ap, AF.Square, r=[x.buf], w=[s.buf])
            for tg in range(ntg):
                self.mm(self.psb[tg], self.psb[tg].ap[:, 0:tgs], self.ones, s.ap[:, tg * tgs:(tg + 1) * tgs],
                        c == 0, c == 15, r=[s.buf, self.cbf.buf])
            self.ts("dve", XT[c].ap, x.ap, vec.ap[:, gcol + c:gcol + c + 1], None, ALU.mult,
                    r=[x.buf, vec.buf], w=[XT[c].buf])
        for tg in range(ntg):
            sl = slice(tg * tgs, (tg + 1) * tgs)
            self.act(rstd.ap[:, sl], self.psb[tg].ap[:, 0:tgs], AF.Sqrt, r=[self.psb[tg].buf, self.epsb.buf],
                     w=[rstd.bufs[tg]], scale=1.0 / D, bias=self.epsb.ap[:, 0:1])
            self.A("dve", lambda e, sl=sl: e.reciprocal(out=rstd.ap[:, sl], in_=rstd.ap[:, sl]),
                   r=[rstd.bufs[tg]], w=[rstd.bufs[tg]])
        for c in range(16):
            self.tt("dve" if c % 2 == 0 else "pool", XT[c].ap, XT[c].ap, rstd.ap, ALU.mult,
                    r=[XT[c].buf] + rstd.bufs, w=[XT[c].buf])

    def load_w(self, wt, W, k0rows, kcs, col0, ncols, dcol=0):
        src = W[k0rows:k0rows + kcs * 128, col0:col0 + ncols].rearrange("(kc p) n -> p kc n", p=128)
        for k0 in range(0, kcs, 4):
            k1 = min(kcs, k0 + 4)
            self.dma("pool", wt.ap[:, k0:k1, dcol:dcol + ncols], src[:, k0:k1, :], w=[wt.bufs[k0 // 4]])

    def bigmm(self, W, kcs, col0, ngroups, grp, xt, ntok, WT, epi, loads=None):
        tgs = min(512, ntok)
        ntg = ntok // tgs
        if loads is None:
            loads = lambda wt, gi: self.load_w(wt, W, 0, kcs, col0 + gi * grp, grp)
        tiles = {}
        tiles[0] = WT.next()
        loads(tiles[0], 0)
        for gi in range(ngroups):
            if gi + 1 < ngroups:
                tiles[gi + 1] = WT.next()
                loads(tiles[gi + 1], gi + 1)
            wt = tiles.pop(gi)
            for m in range(grp // 128):
                for tg in range(ntg):
                    pb = self.nextps()
                    for kc in range(kcs):
                        self.mm(pb, pb.ap[:, 0:tgs], wt.ap[:, kc, m * 128:(m + 1) * 128],
                                xt[kc].ap[:, tg * tgs:(tg + 1) * tgs], kc == 0, kc == kcs - 1,
                                r=[wt.bufs[kc // 4], xt[kc].buf])
                    epi(gi, m, tg, pb)

    def bigmm_tm(self, W, kcs, col0, ngroups, grp, xt, ntok, WT, epi):
        tiles = {}
        tiles[0] = WT.next()
        self.load_w(tiles[0], W, 0, kcs, col0, grp)
        for gi in range(ngroups):
            if gi + 1 < ngroups:
                tiles[gi + 1] = WT.next()
                self.load_w(tiles[gi + 1], W, 0, kcs, col0 + (gi + 1) * grp, grp)
            wt = tiles.pop(gi)
            for tb in range(ntok // 128):
                pb = self.nextps()
                for kc in range(kcs):
                    self.mm(pb, pb.ap[:, 0:grp], xt[kc].ap[:, tb * 128:(tb + 1) * 128], wt.ap[:, kc, :],
                            kc == 0, kc == kcs - 1, r=[wt.bufs[kc // 4], xt[kc].buf])
                epi(gi, tb, pb)

    def phase_inproj(self, L, xsrc):
        P = self.P
        W = self.w_in[L]
        vec = self.vec
        al = self.alloc()
        XT = [al.tile([128, T], BF16) for _ in range(16)]
        WT = al.rot(2, [128, 16, 512], BF16, nb=4)
        tmp0 = al.off
        self.norm(xsrc, V_NMIX, XT, T, al)
        P.barrier()
        if "norm_only" in self.dbg:
            for c in range(8):
                self.dma("sp", self.qdT[c * 128:(c + 1) * 128, :], XT[c].ap, r=[XT[c].buf])
            P.barrier()
            return
        al.off = tmp0
        rope = al.tile([128, 4, T], F32)
        for i in range(4):
            self.dma("sp", rope.ap[:, i, :], self.cst[:, i * T:(i + 1) * T], w=[rope.buf])
        raw = al.rot(2, [128, 512], BF16)
        t1 = al.rot(2, [128, 512], F32)
        t2 = al.rot(2, [128, 512], F32)
        st = al.rot(2, [128, T], BF16)
        sig = al.rot(2, [128, 512], F32)
        vst = al.rot(3, [128, 512], BF16)
        upad = al.rot(2, [128, T + 32], F32)
        acc1 = al.tile([128, T], F32)
        acc2 = al.tile([128, T], F32)
        acc3 = al.tile([128, T], F32)
        for u in upad.tiles:
            self.A("pool", lambda e, u=u: e.memset(u.ap[:, 0:32], 0.0), w=[u.buf])
        cur = {}

        def rope_epi(dst, ci, si, perm):
            def epi(gi, m, tg, pb):
                sl = slice(tg * 512, (tg + 1) * 512)
                if tg == 0:
                    cur["st"] = st.next()
                s_ = cur["st"]
                rw = raw.next()
                a1 = t1.next()
                a2 = t2.next()
                self.copy("act", rw.ap, pb.ap, r=[pb.buf], w=[rw.buf])
                p2 = self.nextps(6, 8)
                self.mm(p2, p2.ap, perm, rw.ap, True, True, r=[rw.buf, self.cbf.buf])
                self.tt("dve", a1.ap, pb.ap, rope.ap[:, ci, sl], ALU.mult, r=[pb.buf, rope.buf, rw.buf], w=[a1.buf])
                self.tt("dve", a2.ap, p2.ap, rope.ap[:, si, sl], ALU.mult, r=[p2.buf, rope.buf], w=[a2.buf])
                self.tt("pool", s_.ap[:, sl], a1.ap, a2.ap, ALU.add, r=[a1.buf, a2.buf], w=[s_.buf])
                if tg == 3:
                    ch = gi * 4 + m
                    self.dma("sp", dst[ch * 128:(ch + 1) * 128, :], s_.ap, r=[s_.buf])
            return epi

        def act_epi(dst, func, row0):
            def epi(gi, m, tg, pb):
                sl = slice(tg * 512, (tg + 1) * 512)
                if tg == 0:
                    cur["st"] = st.next()
                s_ = cur["st"]
                self.act(s_.ap[:, sl], pb.ap, func, r=[pb.buf], w=[s_.buf])
                if tg == 3:
                    ch = row0 + gi * 4 + m
                    self.dma("sp", dst[ch * 128:(ch + 1) * 128, :], s_.ap, r=[s_.buf])
            return epi

        def tm_epi(dst):
            def epi(gi, tb, pb):
                v = vst.next()
                self.copy("act" if tb % 2 == 0 else "dve", v.ap, pb.ap, r=[pb.buf], w=[v.buf])
                self.dma("sp", dst[tb * 128:(tb + 1) * 128, gi * 512:(gi + 1) * 512], v.ap, r=[v.buf])
            return epi

        self.bigmm(W, 16, 0, 2, 512, XT, T, WT, rope_epi(self.qdT, 0, 1, self.permd))
        if "stop_q" in self.dbg:
            P.barrier()
            return
        self.bigmm(W, 16, 1024, 2, 512, XT, T, WT, rope_epi(self.kdT, 0, 1, self.permd))
        self.bigmm_tm(W, 16, 2048, 2, 512, XT, T, WT, tm_epi(self.vd))

        if "stop_v" in self.dbg:
            P.barrier()
            return
        def glu_loads(wt, gi):
            self.load_w(wt, W, 0, 16, 3072 + gi * 256, 256, dcol=0)
            self.load_w(wt, W, 0, 16, 4096 + gi * 256, 256, dcol=256)
        tiles = {0: WT.next()}
        glu_loads(tiles[0], 0)
        for gi in range(4):
            if gi + 1 < 4:
                tiles[gi + 1] = WT.next()
                glu_loads(tiles[gi + 1], gi + 1)
            wt = tiles.pop(gi)
            for m in range(2):
                ch = gi * 2 + m
                up = upad.next()
                for tg in range(4):
                    pa = self.nextps()
                    pg = self.nextps()
                    for kc in range(16):
                        self.mm(pa, pa.ap, wt.ap[:, kc, m * 128:(m + 1) * 128], XT[kc].ap[:, tg * 512:(tg + 1) * 512],
                                kc == 0, kc == 15, r=[wt.bufs[kc // 4], XT[kc].buf])
                    for kc in range(16):
                        self.mm(pg, pg.ap, wt.ap[:, kc, 256 + m * 128:256 + (m + 1) * 128],
                                XT[kc].ap[:, tg * 512:(tg + 1) * 512],
                                kc == 0, kc == 15, r=[wt.bufs[kc // 4], XT[kc].buf])
                    sg = sig.next()
                    self.act(sg.ap, pg.ap, AF.Sigmoid, r=[pg.buf], w=[sg.buf])
                    self.tt("dve", up.ap[:, 32 + tg * 512:32 + (tg + 1) * 512], pa.ap, sg.ap, ALU.mult,
                            r=[pa.buf, sg.buf], w=[up.buf])
                wc = lambda k: vec.ap[:, V_CONVW + ch * 31 + k:V_CONVW + ch * 31 + k + 1]
                self.ts("dve", acc1.ap, up.ap[:, 2:2 + T], wc(0), vec.ap[:, V_CONVB + ch:V_CONVB + ch + 1],
                        ALU.mult, ALU.add, r=[up.buf, vec.buf], w=[acc1.buf])
                for k in range(1, 16):
                    self.stt("dve", acc1.ap, up.ap[:, 2 + k:2 + k + T], wc(k), acc1.ap, ALU.mult, ALU.add,
                             r=[up.buf, acc1.buf], w=[acc1.buf])
                self.ts("pool", acc2.ap, up.ap[:, 2 + 16:2 + 16 + T], wc(16), None, ALU.mult,
                        r=[up.buf, vec.buf], w=[acc2.buf])
                for k in range(17, 31):
                    self.ts("pool", acc3.ap, up.ap[:, 2 + k:2 + k + T], wc(k), None, ALU.mult,
                            r=[up.buf, vec.buf], w=[acc3.buf])
                    self.tt("pool", acc2.ap, acc2.ap, acc3.ap, ALU.add, r=[acc3.buf, acc2.buf], w=[acc2.buf])
                self.tt("pool", acc2.ap, acc1.ap, acc2.ap, ALU.add, r=[acc1.buf, acc2.buf], w=[acc2.buf])
                self.dma("sp", self.ycT[ch * 128:(ch + 1) * 128, :], acc2.ap, r=[acc2.buf])

        if "stop_c" in self.dbg:
            P.barrier()
            return
        self.bigmm(W, 16, 5120, 2, 512, XT, T, WT, rope_epi(self.qrT, 2, 3, self.permr))
        self.bigmm(W, 16, 6144, 2, 512, XT, T, WT, rope_epi(self.krT, 2, 3, self.permr))
        self.bigmm_tm(W, 16, 7168, 2, 512, XT, T, WT, tm_epi(self.vr))
        self.bigmm(W, 16, 8192, 2, 512, XT, T, WT, act_epi(self.sgT, AF.Silu, 0))
        self.bigmm(W, 16, 9216, 12, 512, XT, T, WT, act_epi(self.gT, AF.Sigmoid, 0))
        P.barrier()


    def ln_stats(self, srcs, nfeat, al):
        yb = al.rot(2, [128, T], BF16)
        sq = al.rot(2, [128, T], BF16)
        mu = al.tile([128, T], F32, nb=4)
        rs = al.tile([128, T], F32, nb=4)
        srcs = list(srcs)
        n = len(srcs)
        for i, getter in enumerate(srcs):
            y = getter()
            b = yb.next()
            s = sq.next()
            self.copy("act", b.ap, y.ap, r=[y.buf], w=[b.buf])
            self.tt("pool", s.ap, y.ap, y.ap, ALU.mult, r=[y.buf], w=[s.buf])
            for tg in range(4):
                sl = slice(tg * 512, (tg + 1) * 512)
                self.mm(self.psb[tg], self.psb[tg].ap, self.ones, b.ap[:, sl], i == 0, i == n - 1,
                        r=[b.buf, self.cbf.buf])
                self.mm(self.psb[4 + tg], self.psb[4 + tg].ap, self.ones, s.ap[:, sl], i == 0, i == n - 1,
                        r=[s.buf, self.cbf.buf])
        inv = 1.0 / nfeat
        for tg in range(4):
            sl = slice(tg * 512, (tg + 1) * 512)
            p1 = self.psb[tg]
            p2 = self.psb[4 + tg]
            self.P.op("act", lambda e, sl=sl, p1=p1: e.mul(out=mu.ap[:, sl], in_=p1.ap, mul=inv),
                      [p1.buf], [mu.bufs[tg]])
            self.tt("dve", rs.ap[:, sl], mu.ap[:, sl], mu.ap[:, sl], ALU.mult, r=[mu.bufs[tg]], w=[rs.bufs[tg]])
            self.stt("dve", rs.ap[:, sl], p2.ap, inv, rs.ap[:, sl], ALU.mult, ALU.subtract,
                     r=[p2.buf, rs.bufs[tg]], w=[rs.bufs[tg]])
            self.act(rs.ap[:, sl], rs.ap[:, sl], AF.Sqrt, r=[rs.bufs[tg], self.epsb.buf], w=[rs.bufs[tg]],
                     bias=self.epsb.ap[:, 0:1])
            self.A("dve", lambda e, sl=sl: e.reciprocal(out=rs.ap[:, sl], in_=rs.ap[:, sl]),
                   r=[rs.bufs[tg]], w=[rs.bufs[tg]])
        return mu, rs

    def phase_convln(self, L):
        al = self.alloc()
        vec = self.vec
        ys = al.rot(2, [128, T], F32)
        st = al.rot(2, [128, T], BF16)

        def getter(c):
            def g():
                y = ys.next()
                self.dma("sp", y.ap, self.ycT[c * 128:(c + 1) * 128, :], w=[y.buf])
                return y
            return g
        mu, rs = self.ln_stats([getter(c) for c in range(8)], 1024, al)
        for c in range(8):
            y = ys.next()
            self.dma("sp", y.ap, self.ycT[c * 128:(c + 1) * 128, :], w=[y.buf])
            self.tt("dve", y.ap, y.ap, mu.ap, ALU.subtract, r=[y.buf] + mu.bufs, w=[y.buf])
            self.tt("pool", y.ap, y.ap, rs.ap, ALU.mult, r=[y.buf] + rs.bufs, w=[y.buf])
            s_ = st.next()
            self.act(s_.ap, y.ap, AF.Silu, r=[y.buf, vec.buf], w=[s_.buf],
                     scale=vec.ap[:, V_LNG + c:V_LNG + c + 1], bias=vec.ap[:, V_LNB + c:V_LNB + c + 1])
            self.dma("sp", self.ybrT[1024 + c * 128:1024 + (c + 1) * 128, :], s_.ap, r=[s_.buf])
        self.P.barrier()

    def phase_diffattn(self, L):
        al = self.alloc()
        qt = al.rot(2, [128, T], BF16)
        kt = al.rot(2, [128, T], BF16)
        vt = al.rot(2, [128, 16, 128], BF16)
        pT = al.rot(4, [128, 512], BF16)
        rec = al.rot(2, [128, 512], F32)
        om = [al.rot(2, [128, 512], F32), al.rot(2, [128, 512], F32)]
        yat = al.rot(2, [128, 512], F32)
        sqb = al.rot(2, [128, 512], BF16)
        rr = al.rot(2, [128, 512], F32)
        st = al.rot(2, [128, T], BF16)
        lam = self.lam
        scale = 64.0 ** -0.5
        for h in range(8):
            q = qt.next()
            k = kt.next()
            v = vt.next()
            self.dma("sp", q.ap, self.qdT[h * 128:(h + 1) * 128, :], w=[q.buf])
            self.dma("sp", k.ap, self.kdT[h * 128:(h + 1) * 128, :], w=[k.buf])
            self.dma("sp", v.ap, self.vd[:, h * 128:(h + 1) * 128].rearrange("(tb p) e -> p tb e", p=128),
                     w=[v.buf])
            s_ = st.next()
            for g in range(4):
                oms = []
                for m in range(2):
                    O = self.psb[m]
                    S = self.psb[2 + m]
                    order = [4 * g] + list(range(0, 4 * g)) + [4 * g + 1, 4 * g + 2, 4 * g + 3]
                    for idx, j in enumerate(order):
                        lo = (j - 4 * g) * 128 if j >= 4 * g else 0
                        n = 512 - lo
                        sp = self.nextps(4, 8)
                        self.mm(sp, sp.ap[:, 0:n], k.ap[m * 64:(m + 1) * 64, j * 128:(j + 1) * 128],
                                q.ap[m * 64:(m + 1) * 64, g * 512 + lo:(g + 1) * 512], True, True,
                                r=[k.buf, q.buf])
                        p = pT.next()
                        self.act(p.ap[:, 0:n], sp.ap[:, 0:n], AF.Exp, r=[sp.buf], w=[p.buf], scale=scale)
                        if j >= 4 * g:
                            self.tt("pool", p.ap[:, 0:128], p.ap[:, 0:128], self.tri, ALU.mult,
                                    r=[p.buf, self.cbf.buf], w=[p.buf])
                        first = idx == 0
                        last = idx == len(order) - 1
                        self.mm(O, O.ap[:, lo:512], v.ap[:, j, :], p.ap[:, 0:n], first, last, r=[v.buf, p.buf])
                        self.mm(S, S.ap[:, lo:512], self.ones, p.ap[:, 0:n], first, last, r=[p.buf, self.cbf.buf])
                    rc = rec.next()
                    self.A("dve", lambda e, rc=rc, S=S: e.reciprocal(out=rc.ap, in_=S.ap), r=[S.buf], w=[rc.buf])
                    o_ = om[m].next()
                    self.tt("dve", o_.ap, O.ap, rc.ap, ALU.mult, r=[O.buf, rc.buf], w=[o_.buf])
                    oms.append(o_)
                ya = yat.next()
                self.stt("dve", ya.ap, oms[1].ap, lam.ap[:, 1:2], oms[0].ap, ALU.mult, ALU.add,
                         r=[oms[0].buf, oms[1].buf, lam.buf], w=[ya.buf])
                sb_ = sqb.next()
                self.act(sb_.ap, ya.ap, AF.Square, r=[ya.buf], w=[sb_.buf])
                sp = self.nextps(4, 8)
                self.mm(sp, sp.ap, self.ones, sb_.ap, True, True, r=[sb_.buf, self.cbf.buf])
                r_ = rr.next()
                self.act(r_.ap, sp.ap, AF.Sqrt, r=[sp.buf, self.epsb.buf], w=[r_.buf], scale=1.0 / 128,
                         bias=self.epsb.ap[:, 0:1])
                self.A("dve", lambda e, r_=r_: e.reciprocal(out=r_.ap, in_=r_.ap), r=[r_.buf], w=[r_.buf])
                self.stt("dve", s_.ap[:, g * 512:(g + 1) * 512], ya.ap, lam.ap[:, 2:3], r_.ap, ALU.mult, ALU.mult,
                         r=[ya.buf, r_.buf, lam.buf], w=[s_.buf])
            self.dma("sp", self.ybrT[h * 128:(h + 1) * 128, :], s_.ap, r=[s_.buf])
        self.P.barrier()

    def phase_ret(self, L):
        al = self.alloc()
        vec = self.vec
        dect = al.tile([128, 1024], F32)
        qdec = al.tile([128, 1024], F32)
        kdec = al.tile([128, 8], F32)
        self.dma("sp", dect.ap, self.cst[:, C_DECT:C_DECT + 1024], w=[dect.buf])
        self.dma("sp", qdec.ap, self.cst[:, C_QDEC:C_QDEC + 1024], w=[qdec.buf])
        self.dma("sp", kdec.ap, self.cst[:, C_KDEC:C_KDEC + 8], w=[kdec.buf])
        qt = al.rot(2, [128, T], BF16)
        kt = al.rot(2, [128, T], BF16)
        vt = al.rot(2, [128, 16, 128], BF16)
        sgt = al.rot(2, [128, T], BF16)
        ktm = al.rot(2, [128, 128], BF16)
        aT = al.rot(2, [128, 128], BF16)
        qs = al.rot(2, [128, 128], BF16)
        stf = al.tile([128, 128], F32)
        stb = al.rot(2, [128, 128], BF16)
        yacc = al.tile([128, T], F32)
        st = al.rot(2, [128, T], BF16)
        for h in range(8):
            hs = slice(h * 128, (h + 1) * 128)
            q = qt.next()
            k = kt.next()
            v = vt.next()
            sg = sgt.next()
            self.dma("sp", q.ap, self.qrT[hs, :], w=[q.buf])
            self.dma("sp", k.ap, self.krT[hs, :], w=[k.buf])
            self.dma("sp", v.ap, self.vr[:, hs].rearrange("(tb p) e -> p tb e", p=128), w=[v.buf])
            self.dma("sp", sg.ap, self.sgT[hs, :], w=[sg.buf])
            self.A("pool", lambda e: e.memset(stf.ap, 0.0), w=[stf.buf])
            sb_cur = stb.next()
            self.A("pool", lambda e, t=sb_cur: e.memset(t.ap, 0.0), w=[sb_cur.buf])
            for n in range(16):
                cs = slice(n * 128, (n + 1) * 128)
                sp = self.nextps(0, 4)
                self.mm(sp, sp.ap[:, 0:128], k.ap[:, cs], q.ap[:, cs], True, True, r=[k.buf, q.buf])
                a = aT.next()
                self.tt("dve", a.ap, sp.ap[:, 0:128], dect.ap[:, hs], ALU.mult, r=[sp.buf, dect.buf], w=[a.buf])
                qq = qs.next()
                self.tt("pool", qq.ap, q.ap[:, cs], qdec.ap[:, hs], ALU.mult, r=[q.buf, qdec.buf], w=[qq.buf])
                op_ = self.nextps(4, 6)
                self.mm(op_, op_.ap[:, 0:128], v.ap[:, n, :], a.ap, True, False, r=[v.buf, a.buf])
                self.mm(op_, op_.ap[:, 0:128], sb_cur.ap, qq.ap, False, True, r=[sb_cur.buf, qq.buf])
                self.copy("act", yacc.ap[:, cs], op_.ap[:, 0:128], r=[op_.buf], w=[yacc.buf])
                if n < 15:
                    tp = self.nextps(6, 8)
                    tpb = tp.ap.bitcast(BF16)[:, 0:128]
                    self.P.op("pe", lambda e, tpb=tpb, kin=k.ap[:, cs]: e.transpose(tpb, kin, self.ident),
                              [k.buf, self.cbf.buf], [tp.buf])
                    km = ktm.next()
                    self.ts("dve", km.ap, tpb, kdec.ap[:, h:h + 1], None, ALU.mult, r=[tp.buf, kdec.buf], w=[km.buf])
                    kvp = self.nextps(6, 8)
                    self.mm(kvp, kvp.ap[:, 0:128], km.ap, v.ap[:, n, :], True, True, r=[km.buf, v.buf])
                    self.stt("dve", stf.ap, stf.ap, float(CHUNK_DECAY[h]), kvp.ap[:, 0:128], ALU.mult, ALU.add,
                             r=[stf.buf, kvp.buf], w=[stf.buf])
                    sb_cur = stb.next()
                    self.copy("pool", sb_cur.ap, stf.ap, r=[stf.buf], w=[sb_cur.buf])
            al2 = KB.Alloc(self, al.off)
            mu, rs = self.ln_stats([lambda: yacc], 128, al2)
            self.tt("dve", yacc.ap, yacc.ap, mu.ap, ALU.subtract, r=[yacc.buf] + mu.bufs, w=[yacc.buf])
            self.tt("pool", yacc.ap, yacc.ap, rs.ap, ALU.mult, r=[yacc.buf] + rs.bufs, w=[yacc.buf])
            s_ = st.next()
            self.stt("dve", s_.ap, yacc.ap, vec.ap[:, V_GNG + h:V_GNG + h + 1], sg.ap, ALU.mult, ALU.mult,
                     r=[yacc.buf, vec.buf, sg.buf], w=[s_.buf])
            self.dma("sp", self.ybrT[2048 + h * 128:2048 + (h + 1) * 128, :], s_.ap, r=[s_.buf])
            self.P.barrier()

    def phase_branch(self, L, al, MT):
        Wb = self.w_branch[L]
        YT = al.tile([128, 24, 1024], BF16, nb=24)
        WT = al.rot(2, [128, 24, 512], BF16, nb=6)
        gt = al.rot(2, [128, 3, 1024], BF16)
        mb = [al.rot(2, [128, 512], F32) for _ in range(3)]
        for half in range(2):
            tok0 = half * 1024
            for kc in range(24):
                self.dma("sp", YT.ap[:, kc, :], self.ybrT[kc * 128:(kc + 1) * 128, tok0:tok0 + 1024],
                         w=[YT.bufs[kc]])
            tiles = {0: WT.next()}
            self.load_w(tiles[0], Wb, 0, 24, 0, 512)
            for gi in range(4):
                if gi + 1 < 4:
                    tiles[gi + 1] = WT.next()
                    self.load_w(tiles[gi + 1], Wb, 0, 24, (gi + 1) * 512, 512)
                wt = tiles.pop(gi)
                for m in range(4):
                    c = gi * 4 + m
                    g_ = gt.next()
                    for b in range(3):
                        self.dma("sp", g_.ap[:, b, :],
                                 self.gT[b * 2048 + c * 128:b * 2048 + (c + 1) * 128, tok0:tok0 + 1024], w=[g_.buf])
                    for tgl in range(2):
                        sl = slice(tgl * 512, (tgl + 1) * 512)
                        ms = []
                        for b in range(3):
                            pb = self.nextps()
                            for kc in range(8):
                                self.mm(pb, pb.ap, wt.ap[:, b * 8 + kc, m * 128:(m + 1) * 128],
                                        YT.ap[:, b * 8 + kc, sl], kc == 0, kc == 7,
                                        r=[wt.bufs[(b * 8 + kc) // 4], YT.bufs[b * 8 + kc]])
                            mt_ = mb[b].next()
                            self.tt("dve", mt_.ap, pb.ap, g_.ap[:, b, sl], ALU.mult, r=[pb.buf, g_.buf], w=[mt_.buf])
                            ms.append(mt_)
                        self.tt("pool", ms[0].ap, ms[0].ap, ms[1].ap, ALU.add, r=[ms[0].buf, ms[1].buf], w=[ms[0].buf])
                        self.tt("pool", MT[c].ap[:, tok0 + tgl * 512:tok0 + (tgl + 1) * 512], ms[0].ap, ms[2].ap,
                                ALU.add, r=[ms[0].buf, ms[2].buf], w=[MT[c].buf])
        self.P.barrier()

    def phase_res(self, W, kcs, xt, xsrc, xdst, al, ntok=T, tok0=0, grp=512):
        WT = al.rot(2, [128, kcs, grp], BF16, nb=(kcs + 3) // 4)
        xr = al.rot(2, [128, ntok], F32)
        ntg = ntok // 512
        cur = {}

        def epi(gi, m, tg, pb):
            c = gi * (grp // 128) + m
            sl = slice(tg * 512, (tg + 1) * 512)
            if tg == 0:
                cur["x"] = xr.next()
                self.dma("sp", cur["x"].ap, xsrc[c * 128:(c + 1) * 128, tok0:tok0 + ntok], w=[cur["x"].buf])
            x = cur["x"]
            self.tt("dve", x.ap[:, sl], pb.ap, x.ap[:, sl], ALU.add, r=[pb.buf, x.buf], w=[x.buf])
            if tg == ntg - 1:
                self.dma("sp", xdst[c * 128:(c + 1) * 128, tok0:tok0 + ntok], x.ap, r=[x.buf])
        self.bigmm(W, kcs, 0, D // grp, grp, xt, ntok, WT, epi)
        self.P.barrier()

    def phase_mixer_out(self, L, xsrc):
        al = self.alloc()
        MT = [al.tile([128, T], BF16) for _ in range(16)]
        al2 = KB.Alloc(self, al.off)
        self.phase_branch(L, al2, MT)
        al3 = KB.Alloc(self, al.off)
        self.phase_res(self.w_out[L], 16, MT, xsrc, self.xres, al3)


    def phase_xattn(self, L):
        P = self.P
        al = self.alloc()
        XT = [al.tile([128, T], BF16) for _ in range(16)]
        QT = al.tile([128, 16, T], BF16, nb=16)
        q_end = al.off
        KT = al.tile([128, 16, MEMT], BF16, nb=16)
        VT = al.tile([128, 2, D], BF16, nb=2)
        WT = al.rot(2, [128, 16, 512], BF16, nb=4)
        tmp0 = al.off
        MX = [al.tile([128, MEMT], BF16) for _ in range(16)]
        self.norm(self.memT, V_NMEM, MX, MEMT, al)
        P.barrier()
        Wkv = self.wkv[L]

        def k_epi(gi, m, tg, pb):
            c = gi * 4 + m
            self.copy("act", KT.ap[:, c, :], pb.ap[:, 0:MEMT], r=[pb.buf], w=[KT.bufs[c]])
        self.bigmm(Wkv, 16, 0, 4, 512, MX, MEMT, WT, k_epi)

        def v_epi(gi, tb, pb):
            self.copy("dve", VT.ap[:, tb, gi * 512:(gi + 1) * 512], pb.ap, r=[pb.buf], w=[VT.bufs[tb]])
        self.bigmm_tm(Wkv, 16, D, 4, 512, MX, MEMT, WT, v_epi)
        P.barrier()
        aln = KB.Alloc(self, q_end - 16 * T * 2)
        self.norm(self.xres, V_NXA, XT, T, aln)
        P.barrier()
        al.off = tmp0
        pT = al.rot(4, [128, 512], BF16)
        rec = al.rot(2, [128, 512], F32)

        def q_epi(gi, m, tg, pb):
            c = gi * 4 + m
            self.copy("act" if tg % 2 == 0 else "dve", QT.ap[:, c, tg * 512:(tg + 1) * 512], pb.ap,
                      r=[pb.buf], w=[QT.bufs[c]])
        self.bigmm(self.wq[L], 16, 0, 4, 512, XT, T, WT, q_epi)
        scale = 512.0 ** -0.5
        for h in range(4):
            for g in range(4):
                sl = slice(g * 512, (g + 1) * 512)
                ps_ = []
                for mb_ in range(2):
                    sp = self.nextps(0, 4)
                    for dc in range(4):
                        c = h * 4 + dc
                        self.mm(sp, sp.ap, KT.ap[:, c, mb_ * 128:(mb_ + 1) * 128], QT.ap[:, c, sl], dc == 0, dc == 3,
                                r=[KT.bufs[c], QT.bufs[c]])
                    p = pT.next()
                    self.act(p.ap, sp.ap, AF.Exp, r=[sp.buf], w=[p.buf], scale=scale)
                    ps_.append(p)
                S = self.nextps(4, 6)
                self.mm(S, S.ap, self.ones, ps_[0].ap, True, False, r=[ps_[0].buf, self.cbf.buf])
                self.mm(S, S.ap, self.ones, ps_[1].ap, False, True, r=[ps_[1].buf, self.cbf.buf])
                rc = rec.next()
                self.A("dve", lambda e, rc=rc, S=S: e.reciprocal(out=rc.ap, in_=S.ap), r=[S.buf], w=[rc.buf])
                for ec in range(4):
                    c = h * 4 + ec
                    O = self.nextps(6, 8)
                    self.mm(O, O.ap, VT.ap[:, 0, c * 128:(c + 1) * 128], ps_[0].ap, True, False,
                            r=[VT.bufs[0], ps_[0].buf])
                    self.mm(O, O.ap, VT.ap[:, 1, c * 128:(c + 1) * 128], ps_[1].ap, False, True,
                            r=[VT.bufs[1], ps_[1].buf])
                    self.tt("dve", XT[c].ap[:, sl], O.ap, rc.ap, ALU.mult, r=[O.buf, rc.buf], w=[XT[c].buf])
        P.barrier()
        al3 = KB.Alloc(self, q_end - 16 * T * 2)
        self.phase_res(self.wo[L], 16, XT, self.xres, self.xres, al3)

    def phase_ffn(self, L):
        P = self.P
        al = self.alloc()
        XT = [al.tile([128, T], BF16) for _ in range(16)]
        WT = al.rot(2, [128, 16, 512], BF16, nb=4)
        tmp0 = al.off
        self.norm(self.xres, V_NFFN, XT, T, al)
        P.barrier()
        al.off = tmp0
        sig = al.rot(2, [128, 512], F32)
        st = al.rot(2, [128, T], BF16)
        W = self.w13[L]

        def loads(wt, gi):
            self.load_w(wt, W, 0, 16, gi * 256, 256, dcol=0)
            self.load_w(wt, W, 0, 16, FFN + gi * 256, 256, dcol=256)
        tiles = {0: WT.next()}
        loads(tiles[0], 0)
        NG = FFN // 256
        for gi in range(NG):
            if gi + 1 < NG:
                tiles[gi + 1] = WT.next()
                loads(tiles[gi + 1], gi + 1)
            wt = tiles.pop(gi)
            for m in range(2):
                ch = gi * 2 + m
                s_ = st.next()
                for tg in range(4):
                    sl = slice(tg * 512, (tg + 1) * 512)
                    pa = self.nextps()
                    pg = self.nextps()
                    for kc in range(16):
                        self.mm(pa, pa.ap, wt.ap[:, kc, m * 128:(m + 1) * 128], XT[kc].ap[:, sl],
                                kc == 0, kc == 15, r=[wt.bufs[kc // 4], XT[kc].buf])
                    for kc in range(16):
                        self.mm(pg, pg.ap, wt.ap[:, kc, 256 + m * 128:256 + (m + 1) * 128], XT[kc].ap[:, sl],
                                kc == 0, kc == 15, r=[wt.bufs[kc // 4], XT[kc].buf])
                    sg = sig.next()
                    self.act(sg.ap, pa.ap, AF.Silu, r=[pa.buf], w=[sg.buf])
                    self.tt("dve", s_.ap[:, sl], pg.ap, sg.ap, ALU.mult, r=[pg.buf, sg.buf], w=[s_.buf])
                self.dma("sp", self.hffT[ch * 128:(ch + 1) * 128, :], s_.ap, r=[s_.buf])
        P.barrier()
        for half in range(2):
            al = self.alloc()
            tok0 = half * 1024
            GT = [al.tile([128, 1024], BF16) for _ in range(44)]
            for kc in range(44):
                self.dma("sp", GT[kc].ap, self.hffT[kc * 128:(kc + 1) * 128, tok0:tok0 + 1024], w=[GT[kc].buf])
            self.phase_res(self.w2[L], 44, GT, self.xres, self.xres, al, ntok=1024, tok0=tok0, grp=256)

    def phase_final(self):
        al = self.alloc()
        vec = self.vec
        xs = al.rot(2, [128, T], F32)
        sq = al.rot(2, [128, T], BF16)
        rstd = al.tile([128, T], F32, nb=4)
        for c in range(16):
            x = xs.next()
            s = sq.next()
            self.dma("sp", x.ap, self.xres[c * 128:(c + 1) * 128, :], w=[x.buf])
            self.act(s.ap, x.ap, AF.Square, r=[x.buf], w=[s.buf])
            for tg in range(4):
                self.mm(self.psb[tg], self.psb[tg].ap, self.ones, s.ap[:, tg * 512:(tg + 1) * 512],
                        c == 0, c == 15, r=[s.buf, self.cbf.buf])
        for tg in range(4):
            sl = slice(tg * 512, (tg + 1) * 512)
            self.act(rstd.ap[:, sl], self.psb[tg].ap, AF.Sqrt, r=[self.psb[tg].buf, self.epsb.buf],
                     w=[rstd.bufs[tg]], scale=1.0 / D, bias=self.epsb.ap[:, 0:1])
            self.A("dve", lambda e, sl=sl: e.reciprocal(out=rstd.ap[:, sl], in_=rstd.ap[:, sl]),
                   r=[rstd.bufs[tg]], w=[rstd.bufs[tg]])
        for c in range(16):
            x = xs.next()
            self.dma("sp", x.ap, self.xres[c * 128:(c + 1) * 128, :], w=[x.buf])
            self.stt("dve", x.ap, x.ap, vec.ap[:, V_NFIN + c:V_NFIN + c + 1], rstd.ap, ALU.mult, ALU.mult,
                     r=[x.buf, vec.buf] + rstd.bufs, w=[x.buf])
            self.dma("sp", self.yT[c * 128:(c + 1) * 128, :], x.ap, r=[x.buf])
        self.P.barrier()

    def build_all(self):
        self.setup()
        for L in range(self.depth):
            self.layer_setup(L)
            xsrc = self.xT if L == 0 else self.xres
            self.phase_inproj(L, xsrc)
            self.phase_convln(L)
            self.phase_diffattn(L)
            self.phase_ret(L)
            self.phase_mixer_out(L, xsrc)
            self.phase_xattn(L)
            self.phase_ffn(L)
        self.phase_final()
        self.P.emit()


def _pcols(v):
    return np.ascontiguousarray(np.asarray(v, np.float32).reshape(-1, 128).T)


def prep_shared(inp):
    f32 = np.float32
    vecs = np.zeros((DEPTH, 128, NVEC), f32)
    lamin = np.zeros((DEPTH, 128, 256), f32)
    for L in range(DEPTH):
        vecs[L, :, V_NMIX:V_NMIX + 16] = _pcols(inp["norm_mix"][L])
        vecs[L, :, V_NXA:V_NXA + 16] = _pcols(inp["norm_xattn"][L])
        vecs[L, :, V_NMEM:V_NMEM + 16] = _pcols(inp["norm_mem"][L])
        vecs[L, :, V_NFFN:V_NFFN + 16] = _pcols(inp["norm_ffn"][L])
        cw = np.asarray(inp["conv_w"][L], f32)
        vecs[L, :, V_CONVW:V_CONVW + 248] = cw.reshape(31, 8, 128).transpose(2, 1, 0).reshape(128, 248)
        vecs[L, :, V_CONVB:V_CONVB + 8] = _pcols(inp["conv_b"][L])
        vecs[L, :, V_LNG:V_LNG + 8] = _pcols(inp["conv_ln_g"][L])
        vecs[L, :, V_LNB:V_LNB + 8] = _pcols(inp["conv_ln_b"][L])
        vecs[L, :, V_GNG:V_GNG + 8] = _pcols(inp["ret_gn_g"][L])
        vecs[L, :, V_SUBLN] = np.asarray(inp["diff_subln"][L], f32)
        vecs[L, :, V_NFIN:V_NFIN + 16] = _pcols(inp["norm_final"])
        dl = np.asarray(inp["diff_lambda"][L], f32)
        lamin[L, :, :] = np.concatenate([dl[0], dl[2], dl[1], dl[3]])[None, :]
    sh = {
        "cst": host_consts(), "vecs": vecs, "lamin": lamin,
        "w_in": np.ascontiguousarray(inp["w_in"], f32),
        "w_branch": np.ascontiguousarray(np.asarray(inp["w_branch"], f32).reshape(DEPTH, 3 * MIXW, D)),
        "w_out": np.ascontiguousarray(inp["w_out"], f32),
        "xattn_wq": np.ascontiguousarray(inp["xattn_wq"], f32),
        "xattn_wkv": np.ascontiguousarray(inp["xattn_wkv"], f32),
        "xattn_wo": np.ascontiguousarray(inp["xattn_wo"], f32),
        "ffn_w13": np.ascontiguousarray(inp["ffn_w13"], f32),
        "ffn_w2": np.ascontiguousarray(inp["ffn_w2"], f32),
    }
    return sh


def prep_core(inp, b, sh):
    m = dict(sh)
    m["xT"] = np.ascontiguousarray(np.asarray(inp["x"][b], np.float32).T)
    m["memT"] = np.ascontiguousarray(np.asarray(inp["mem"][b], np.float32).T)
    return m


_NC_CACHE = {}


def kernel(**inputs):
    if "nc" not in _NC_CACHE:
        nc = bass.Bass("TRN2", target_bir_lowering=False)
        kb = KB(nc)
        kb.build_all()
        _NC_CACHE["nc"] = nc
    nc = _NC_CACHE["nc"]
    sh = prep_shared(inputs)
    in_maps = [prep_core(inputs, b, sh) for b in range(8)]
    res = run_bass_kernel_spmd(nc, in_maps, core_ids=list(range(8)))
    out = np.stack([np.ascontiguousarray(np.asarray(r["yT"], np.float32).T) for r in res.results], axis=0)
    return out
```
